# Optimizing a Trainium2 kernel written in Bass

```python
import math
import jax, jax.numpy as jnp
from jax import lax
import numpy as np

D_MODEL = 1024
BATCH = 16
SEQ = 4096
DEPTH = 2
DEC_BATCH = 8
DEC_SEQ = 2048
PAST_LEN = 128

N_BRANCH = 4
BRANCH_W = D_MODEL // 4
N_GROUPS = 4
GROUP_DIM = BRANCH_W // N_GROUPS
D_FF = 2816
CHUNK = 128
POOL_WINDOWS = (2, 4, 8, 16)
CONV_WIDTH = 3
LN_EPS = 1e-5
ALPHA = (2.0 * DEPTH) ** 0.25
BETA = (8.0 * DEPTH) ** -0.25
IN_SPLITS = (BRANCH_W, 2 * BRANCH_W, 3 * BRANCH_W, 4 * BRANCH_W, 6 * BRANCH_W)
IN_WIDTH = 7 * BRANCH_W

kernel_name = "hybrid_gated_parallel_encoder"


def _layer_norm(x, g, b):
    xf = x.astype(jnp.float32)
    mu = jnp.mean(xf, axis=-1, keepdims=True)
    var = jnp.mean(jnp.square(xf - mu), axis=-1, keepdims=True)
    y = (xf - mu) * lax.rsqrt(var + LN_EPS)
    return (y * g.astype(jnp.float32) + b.astype(jnp.float32)).astype(x.dtype)


def _swiglu(x, w1, w3, w2):
    return (jax.nn.silu(x @ w1) * (x @ w3)) @ w2


def _short_conv(h, w):
    hp = jnp.pad(h, ((0, 0), (1, 1), (0, 0)))
    return hp[:, :-2] * w[0] + hp[:, 1:-1] * w[1] + hp[:, 2:] * w[2]


def _fourier(z):
    b, s, _ = z.shape
    zh = z.reshape(b, s, N_GROUPS, GROUP_DIM).astype(jnp.float32)
    f = jnp.fft.fft2(zh, axes=(1, 3), norm="ortho").real
    return f.reshape(b, s, BRANCH_W).astype(z.dtype)


def _spatial_gate(uv, g, bn, w_s, b_s):
    u, v = jnp.split(uv, 2, axis=-1)
    v = _layer_norm(v, g, bn)
    b, s, _ = v.shape
    vc = v.reshape(b, s // CHUNK, CHUNK, N_GROUPS, GROUP_DIM)
    mixed = jnp.einsum("hpq,bnqhc->bnphc", w_s, vc) + b_s.T[:, :, None]
    return u * mixed.reshape(b, s, BRANCH_W)


def _multiscale_pool(z, w_grp, scale):
    b, s, _ = z.shape
    zf = z.reshape(b, s, N_GROUPS, GROUP_DIM).astype(jnp.float32)
    csum = jnp.pad(jnp.cumsum(zf, axis=1), ((0, 0), (1, 0), (0, 0), (0, 0)))
    t = jnp.arange(s)
    outs = []
    for gi, win in enumerate(POOL_WINDOWS):
        half = win // 2
        hi = jnp.minimum(t + half, s)
        lo = jnp.maximum(t - half, 0)
        cg = csum[:, :, gi]
        tot = jnp.take(cg, hi, axis=1) - jnp.take(cg, lo, axis=1)
        mean = tot / (hi - lo).astype(jnp.float32)[:, None]
        outs.append(mean - zf[:, :, gi])
    pooled = jnp.stack(outs, axis=2).astype(z.dtype)
    mixed = jnp.einsum("bsgc,gcd->bsgd", pooled, w_grp)
    return mixed.reshape(b, s, BRANCH_W) * scale


def _token_mix(x, w_in, conv_w, sgu_ln_g, sgu_ln_b, sgu_w, sgu_b, pool_w, pool_scale,
               w_gate, b_gate, w_branch, w_out):
    z = x @ w_in
    h, gb, gc, zf, uv, zp = jnp.split(z, IN_SPLITS, axis=-1)
    branches = (
        gb * _short_conv(gc * h, conv_w),
        _fourier(zf),
        _spatial_gate(jax.nn.gelu(uv), sgu_ln_g, sgu_ln_b, sgu_w, sgu_b),
        _multiscale_pool(zp, pool_w, pool_scale),
    )
    merged = jnp.zeros_like(x)
    for i, br in enumerate(branches):
        gate = jax.nn.sigmoid(x @ w_gate[i] + b_gate[i])
        merged = merged + gate * (br @ w_branch[i])
    return merged @ w_out


def _trunk(x, ln_g, ln_b, ffn_w1, ffn_w3, ffn_w2, w_in, conv_w, sgu_ln_g, sgu_ln_b, sgu_w,
           sgu_b, pool_w, pool_scale, w_gate, b_gate, w_branch, w_out):
    for l in range(DEPTH):
        f1 = _swiglu(x, ffn_w1[l, 0], ffn_w3[l, 0], ffn_w2[l, 0])
        x = _layer_norm(ALPHA * x + 0.5 * f1, ln_g[l, 0], ln_b[l, 0])
        m = _token_mix(x, w_in[l], conv_w[l], sgu_ln_g[l], sgu_ln_b[l], sgu_w[l], sgu_b[l],
                       pool_w[l], pool_scale[l], w_gate[l], b_gate[l], w_branch[l], w_out[l])
        x = _layer_norm(ALPHA * x + m, ln_g[l, 1], ln_b[l, 1])
        f2 = _swiglu(x, ffn_w1[l, 1], ffn_w3[l, 1], ffn_w2[l, 1])
        x = _layer_norm(ALPHA * x + 0.5 * f2, ln_g[l, 2], ln_b[l, 2])
    return x


def setup_inputs(seed: int = 0) -> dict:
    key = jax.random.key(seed)
    ks = jax.random.split(key, 20)
    f32 = jnp.float32
    nrm = lambda k, shape, s: jax.random.normal(k, shape, f32) * s
    return {
        "x_prompt": nrm(ks[0], (BATCH, SEQ, D_MODEL), 1.0),
        "x_sample": nrm(ks[1], (DEC_BATCH, DEC_SEQ, D_MODEL), 1.0),
        "ln_g": 1.0 + nrm(ks[2], (DEPTH, 3, D_MODEL), 0.01),
        "ln_b": nrm(ks[3], (DEPTH, 3, D_MODEL), 0.01),
        "ffn_w1": nrm(ks[4], (DEPTH, 2, D_MODEL, D_FF), D_MODEL ** -0.5),
        "ffn_w3": nrm(ks[5], (DEPTH, 2, D_MODEL, D_FF), D_MODEL ** -0.5),
        "ffn_w2": nrm(ks[6], (DEPTH, 2, D_FF, D_MODEL), BETA * D_FF ** -0.5),
        "w_in": nrm(ks[7], (DEPTH, D_MODEL, IN_WIDTH), D_MODEL ** -0.5),
        "conv_w": nrm(ks[8], (DEPTH, CONV_WIDTH, BRANCH_W), CONV_WIDTH ** -0.5),
        "sgu_ln_g": 1.0 + nrm(ks[9], (DEPTH, BRANCH_W), 0.01),
        "sgu_ln_b": nrm(ks[10], (DEPTH, BRANCH_W), 0.01),
        "sgu_w": nrm(ks[11], (DEPTH, N_GROUPS, CHUNK, CHUNK), CHUNK ** -0.5),
        "sgu_b": 1.0 + nrm(ks[12], (DEPTH, N_GROUPS, CHUNK), 0.01),
        "pool_w": nrm(ks[13], (DEPTH, N_GROUPS, GROUP_DIM, GROUP_DIM), GROUP_DIM ** -0.5),
        "pool_scale": 1.0 + nrm(ks[14], (DEPTH, BRANCH_W), 0.01),
        "w_gate": nrm(ks[15], (DEPTH, N_BRANCH, D_MODEL, D_MODEL), D_MODEL ** -0.5),
        "b_gate": nrm(ks[16], (DEPTH, N_BRANCH, D_MODEL), 0.01),
        "w_branch": nrm(ks[17], (DEPTH, N_BRANCH, BRANCH_W, D_MODEL), BETA * BRANCH_W ** -0.5),
        "w_out": nrm(ks[18], (DEPTH, D_MODEL, D_MODEL), BETA * D_MODEL ** -0.5),
    }


def reference(x_prompt, x_sample, ln_g, ln_b, ffn_w1, ffn_w3, ffn_w2, w_in, conv_w, sgu_ln_g,
              sgu_ln_b, sgu_w, sgu_b, pool_w, pool_scale, w_gate, b_gate, w_branch, w_out):
    y_prompt = _trunk(x_prompt, ln_g, ln_b, ffn_w1, ffn_w3, ffn_w2, w_in, conv_w, sgu_ln_g,
                      sgu_ln_b, sgu_w, sgu_b, pool_w, pool_scale, w_gate, b_gate, w_branch, w_out)
    y_sample = _trunk(x_sample, ln_g, ln_b, ffn_w1, ffn_w3, ffn_w2, w_in, conv_w, sgu_ln_g,
                      sgu_ln_b, sgu_w, sgu_b, pool_w, pool_scale, w_gate, b_gate, w_branch, w_out)
    return (y_prompt, y_sample)
```

```python
import math
import numpy as np
import concourse.bass as bass
import concourse.mybir as mybir
from concourse.bass_utils import run_bass_kernel_spmd

F32 = mybir.dt.float32
BF16 = mybir.dt.bfloat16
AF = mybir.ActivationFunctionType
ALU = mybir.AluOpType

NCORES = 8
D = 1024
KC = 8
DFF = 2816
HCN = 22
DEPTH = 2
TB = 512
NTOK = 10240
NBLK = NTOK // TB
SEQS = [(0, 4096), (4096, 4096), (8192, 2048)]
ALPHA = (2.0 * DEPTH) ** 0.25
LN_EPS = 1e-5
EPS1 = LN_EPS / (ALPHA * ALPHA)
RES_SCALE = 0.5 / ALPHA
GELU_C = math.sqrt(2.0 / math.pi)
NSLOT = 7
SLOT_E = 2048

DEBUG = False

COMPUTE = ("pe", "act", "dve", "pool")
NSEM = 8
EPOCH = 2048
KDMA = 8


class Op:
    __slots__ = ("eng", "fn", "reads", "writes", "guards", "is_dma", "idx", "deps", "observed",
                 "ticket", "dma_slot", "dma_use", "waits")

    def __init__(self, eng, fn, reads, writes, guards, is_dma):
        self.eng = eng
        self.fn = fn
        self.reads = tuple(reads)
        self.writes = tuple(writes)
        self.guards = tuple(guards)
        self.is_dma = is_dma
        self.deps = []
        self.observed = False
        self.ticket = None
        self.waits = []


class Sched:
    def __init__(self, nc, same_engine_sync=True):
        self.nc = nc
        self.ops = []
        self.same_engine_sync = same_engine_sync
        self.final_dma = []

    def op(self, eng, fn, reads=(), writes=(), guards=()):
        ps = [k for k in reads if isinstance(k, tuple) and k[0] == "PS" and k not in writes]
        if ps:
            writes = list(writes) + ps
        o = Op(eng, fn, reads, writes, guards, False)
        o.idx = len(self.ops)
        self.ops.append(o)
        return o

    def dma(self, queue, fn, reads=(), writes=(), guards=(), final=False):
        o = Op(queue, fn, reads, writes, guards, True)
        o.idx = len(self.ops)
        self.ops.append(o)
        if final:
            self.final_dma.append(o)
        return o

    def barrier(self):
        o = Op("barrier", None, (), (), (), False)
        o.idx = len(self.ops)
        self.ops.append(o)

    def _analyze(self):
        ops = self.ops
        last_writer = {}
        readers = {}
        engines = list(COMPUTE) + ["sp"]
        since_latest = {}
        since_dma = []
        fence = []
        fence_applied = {e: 0 for e in engines}
        fence_gen = 0
        for o in ops:
            if o.eng == "barrier":
                fence = list(since_latest.values()) + list(since_dma)
                fence_gen += 1
                since_latest = {}
                since_dma = []
                continue
            raw = set()
            other = set()
            if fence_applied[o.eng] != fence_gen:
                fence_applied[o.eng] = fence_gen
                raw.update(fence)
            for r in o.reads:
                w = last_writer.get(r)
                if w is not None:
                    raw.add(w)
            for w_ in o.writes:
                w = last_writer.get(w_)
                if w is not None:
                    other.add(w)
                for rd in readers.get(w_, ()):
                    other.add(rd)
            raw.discard(o.idx)
            other.discard(o.idx)
            o.deps = (sorted(raw), sorted(other - raw))
            for r in o.reads + o.guards:
                lst = readers.setdefault(r, [])
                if not o.is_dma:
                    lst[:] = [x for x in lst if ops[x].is_dma or ops[x].eng != o.eng]
                lst.append(o.idx)
            for w_ in o.writes:
                last_writer[w_] = o.idx
                readers[w_] = []
            if o.is_dma:
                since_dma.append(o.idx)
            else:
                since_latest[o.eng] = o.idx
        waited = {e: {p: -1 for p in COMPUTE} for e in engines}
        waited_dma = {e: set() for e in engines}
        for o in ops:
            if o.eng == "barrier":
                continue
            need = {}
            for kind, lst in (("raw", o.deps[0]), ("other", o.deps[1])):
                for d in lst:
                    p = ops[d]
                    if p.is_dma:
                        if d in waited_dma[o.eng]:
                            continue
                        waited_dma[o.eng].add(d)
                        o.waits.append(("dma", d))
                    else:
                        if p.eng == o.eng and not o.is_dma:
                            if p.eng == "pe" or not self.same_engine_sync or kind != "raw":
                                continue
                        if d <= waited[o.eng][p.eng]:
                            continue
                        if d > need.get(p.eng, -1):
                            need[p.eng] = d
            for pe_, d in need.items():
                waited[o.eng][pe_] = d
                ops[d].observed = True
                o.waits.append(("cmp", d))
        cnt = {e: 0 for e in COMPUTE}
        for o in ops:
            if o.eng == "barrier" or o.is_dma:
                continue
            if o.observed:
                o.ticket = cnt[o.eng]
                cnt[o.eng] += 1
        dcnt = {}
        for o in ops:
            if o.is_dma:
                n = dcnt.get(o.eng, 0)
                dcnt[o.eng] = n + 1
                o.dma_slot = n % KDMA
                o.dma_use = n // KDMA + 1
        self.queues = sorted(dcnt.keys())

    def emit(self):
        nc = self.nc
        self._analyze()
        sems = {e: [nc.alloc_semaphore(f"s_{e}_{i}") for i in range(NSEM)] for e in COMPUTE}
        dsems = {q: [nc.alloc_semaphore(f"d_{q}_{i}") for i in range(KDMA)] for q in self.queues}
        ops = self.ops
        by_eng = {e: [] for e in list(COMPUTE) + ["sp"]}
        for o in ops:
            if o.eng != "barrier":
                by_eng[o.eng].append(o)

        def cmp_wait(p):
            k = p.ticket
            ep = k // EPOCH
            return sems[p.eng][ep % NSEM], (ep // NSEM) * EPOCH + (k % EPOCH) + 1

        def run(engname, eng):
            for o in by_eng[engname]:
                for kind, d in o.waits:
                    p = ops[d]
                    if kind == "dma":
                        eng.wait_ge(dsems[p.eng][p.dma_slot], 16 * p.dma_use)
                    else:
                        s, v = cmp_wait(p)
                        eng.wait_ge(s, v)
                if o.is_dma:
                    if o.dma_use > 1:
                        eng.wait_ge(dsems[o.eng][o.dma_slot], 16 * (o.dma_use - 1))
                    ins = o.fn(eng)
                    ins.then_inc(dsems[o.eng][o.dma_slot], 16)
                else:
                    ins = o.fn(eng)
                    if o.ticket is not None:
                        ins.then_inc(sems[o.eng][(o.ticket // EPOCH) % NSEM], 1)
            if engname == "sp":
                for o in self.final_dma:
                    eng.wait_ge(dsems[o.eng][o.dma_slot], 16 * o.dma_use)

        with nc.Block() as block:
            @block.tensor
            def _(e):
                run("pe", e)

            @block.scalar
            def _(e):
                run("act", e)

            @block.vector
            def _(e):
                run("dve", e)

            @block.gpsimd
            def _(e):
                run("pool", e)

            @block.sync
            def _(e):
                run("sp", e)
        return {e: len(v) for e, v in by_eng.items()}


def _fourier_tables(S):
    N2 = S // 128
    P2 = 128 // N2
    s1 = np.arange(128)[:, None, None]
    s2 = np.arange(N2)[None, :, None]
    k1 = np.arange(128)[None, None, :]
    th = 2 * np.pi * ((k1 * (N2 * s1 + s2)) % S) / S
    T1 = np.stack([np.cos(th), -np.sin(th)], axis=2).reshape(128, N2 * 256)
    T2 = np.zeros((2, N2, P2, P2, 2, N2))
    a = np.arange(N2)
    th2 = 2 * np.pi * ((a[:, None] * a[None, :]) % N2) / N2
    for j in range(P2):
        T2[0, :, j, j, 0, :] = np.cos(th2)
        T2[0, :, j, j, 1, :] = -np.sin(th2)
        T2[1, :, j, j, 0, :] = np.sin(th2)
        T2[1, :, j, j, 1, :] = np.cos(th2)
    T2 = T2.reshape(2, 128, 256)
    return T1.astype(np.float32), T2.astype(np.float32)


def _const_inputs():
    c = {}
    c["ident"] = np.eye(128, dtype=np.float32)
    for S in (4096, 2048):
        t1, t2 = _fourier_tables(S)
        c[f"t1_{S}"] = t1
        c[f"t2_{S}"] = t2
    cc = np.arange(64)
    th3 = 2 * np.pi * ((cc[:, None] * cc[None, :]) % 64) / 64
    c["bdc"] = np.kron(np.eye(2), np.cos(th3)).astype(np.float32)
    c["bds"] = np.kron(np.eye(2), np.sin(th3)).astype(np.float32)
    pc = np.ones((128, 2, 2, 8), dtype=np.float32)
    for j in range(2):
        for p in range(128):
            H = [[1, 2], [4, 8]][j][p // 64]
            for t in range(8):
                if t < H:
                    pc[p, j, 0, t] = 2.0 * H / (t + H)
                r = 7 - t
                if r < H - 1:
                    pc[p, j, 1, t] = 2.0 * H / (r + 1 + H)
    c["poolc"] = pc.reshape(128, 32)
    return c


class Builder:
    def __init__(self):
        self.nc = bass.Bass("TRN2", target_bir_lowering=False)
        self.S = Sched(self.nc)
        self.bank = 0
        self.ring_i = 0
        self.cnt = {}
        self.declare()
        self.alloc()

    @property
    def X(self):
        return self.Xs[self.xpar]

    def rr(self, name, n):
        v = self.cnt.get(name, 0)
        self.cnt[name] = v + 1
        return v % n

    def pbank(self):
        b = self.bank
        self.bank = (self.bank + 1) % 6
        return b

    def declare(self):
        nc = self.nc

        def inp(name, shape):
            return nc.dram_tensor(name, list(shape), F32, kind="ExternalInput").ap()

        def scr(name, shape, dt, dbg=False):
            kind = "ExternalOutput" if (dbg and DEBUG) else "Internal"
            return nc.dram_tensor(name, list(shape), dt, kind=kind).ap()

        self.xin = inp("xin", (NTOK, D))
        self.ln_g = inp("ln_g", (2, 3, D))
        self.ln_b = inp("ln_b", (2, 3, D))
        self.ffn_w1 = inp("ffn_w1", (2, 2, D, DFF))
        self.ffn_w3 = inp("ffn_w3", (2, 2, D, DFF))
        self.ffn_w2 = inp("ffn_w2", (2, 2, DFF, D))
        self.w_in = inp("w_in", (2, D, 1792))
        self.conv_w = inp("conv_w", (2, 3, 256))
        self.sgu_ln_g = inp("sgu_ln_g", (2, 256))
        self.sgu_ln_b = inp("sgu_ln_b", (2, 256))
        self.sgu_w = inp("sgu_w", (2, 4, 128, 128))
        self.sgu_b = inp("sgu_b", (2, 4, 128))
        self.pool_w = inp("pool_w", (2, 4, 64, 64))
        self.pool_scale = inp("pool_scale", (2, 256))
        self.w_gate = inp("w_gate", (2, 4, D, D))
        self.b_gate = inp("b_gate", (2, 4, D))
        self.w_branch = inp("w_branch", (2, 4, 256, D))
        self.w_out = inp("w_out", (2, D, D))
        self.c_ident = inp("ident", (128, 128))
        self.c_t1 = {4096: inp("t1_4096", (128, 8192)), 2048: inp("t1_2048", (128, 4096))}
        self.c_t2 = {4096: inp("t2_4096", (2, 128, 256)), 2048: inp("t2_2048", (2, 128, 256))}
        self.c_bdc = inp("bdc", (128, 128))
        self.c_bds = inp("bds", (128, 128))
        self.c_poolc = inp("poolc", (128, 32))
        self.yout = nc.dram_tensor("yout", [NTOK, D], F32, kind="ExternalOutput").ap()
        self.ws_w1 = scr("ws_w1", (2, 2, 11, 128, 2048), BF16)
        self.ws_w3 = scr("ws_w3", (2, 2, 11, 128, 2048), BF16)
        self.ws_w2 = scr("ws_w2", (2, 2, 16, 128, 1408), BF16)
        self.ws_win = scr("ws_win", (2, 7, 128, 2048), BF16)
        self.ws_wg = scr("ws_wg", (2, 4, 4, 128, 2048), BF16)
        self.ws_wbr = scr("ws_wbr", (2, 4, 128, 2048), BF16)
        self.ws_wo = scr("ws_wo", (2, 4, 128, 2048), BF16)
        self.ws_t1 = {4096: scr("ws_t1_4096", (128, 8192), BF16), 2048: scr("ws_t1_2048", (128, 4096), BF16)}
        self.xs = [scr(f"xs{l}", (KC, 128, NTOK), F32, dbg=True) for l in range(2)]
        self.xsb = [scr(f"xsb{l}", (KC, 128, NTOK), BF16) for l in range(2)]
        self.zfs = [scr(f"zfs{l}", (NTOK, 256), BF16, dbg=True) for l in range(2)]
        self.brfs = [scr(f"brfs{l}", (2, 128, NTOK), BF16, dbg=True) for l in range(2)]

    def alloc(self):
        nc = self.nc
        ARENA_B = 97024
        self.arena = nc.alloc_sbuf_tensor("arena", [128, ARENA_B // 4], F32)

        def view(off, nbytes, dt):
            a = self.arena[:, off // 4:(off + nbytes) // 4]
            if dt == BF16:
                a = a.bitcast(BF16)
            return a

        def v3(off, k, n, dt):
            nb = k * n * (4 if dt == F32 else 2)
            return view(off, nb, dt).rearrange("p (k n) -> p k n", k=k)

        self.Xs = [v3(0, 8, 512, F32), v3(47104, 8, 512, F32)]
        self.xpar = 0
        self.XBa = v3(16384, 8, 512, BF16)
        self.G = v3(24576, 22, 512, BF16)
        self.MERGED = v3(24576, 8, 512, F32)
        self.ACC = v3(40960, 2, 512, F32)
        self.PL = v3(45056, 2, 512, BF16)
        self.H = v3(47104, 2, 528, F32)
        self.GC = v3(51328, 2, 528, F32)
        self.ZP = v3(55552, 2, 528, F32)
        self.GB = v3(59776, 2, 512, F32)
        self.U = v3(63872, 2, 512, F32)
        self.PA = v3(67968, 2, 528, F32)
        self.PB = v3(72192, 2, 528, F32)
        self.PC = v3(76416, 1, 528, F32)
        self.PD = v3(78528, 1, 528, F32)
        self.BRA = v3(80640, 2, 512, BF16)
        self.BRC = v3(82688, 2, 512, BF16)
        self.BRD = v3(84736, 2, 512, BF16)
        self.BRF = v3(86784, 2, 512, BF16)
        self.MERGEDB = v3(88832, 8, 512, BF16)
        self.ZTM = view(0, 16384, BF16)
        self.T1 = view(16384, 16384, BF16)
        self.YL = view(32768, 16384, BF16)
        self.YT = view(49152, 16384, BF16)
        self.XF = view(65536, 16384, BF16)
        self.FO = view(81920, 8192, BF16)

        def sb(name, shape, dt):
            return nc.alloc_sbuf_tensor(name, list(shape), dt)

        self.RING = [sb(f"ring{i}", (128, SLOT_E), BF16) for i in range(NSLOT)]
        self.SQ = [sb(f"sq{i}", (128, 512), BF16) for i in range(2)]
        self.YB = [sb(f"yb{i}", (128, 512), BF16) for i in range(2)]
        self.ONESB = sb("onesb", (128, 128), BF16)
        self.TT = [sb(f"tt{i}", (128, 512), F32) for i in range(2)]
        self.VAR = sb("var", (128, 512), F32)
        self.RSTD = sb("rstd", (128, 512), F32)
        self.SIL = [sb(f"sil{i}", (128, 512), F32) for i in range(2)]
        self.GT = [sb(f"gt{i}", (128, 512), F32) for i in range(2)]
        self.PRD = [sb(f"prd{i}", (128, 512), F32) for i in range(2)]
        self.XIO = [sb(f"xio{i}", (128, 1024), F32) for i in range(3)]
        self.XBHs = [sb(f"xbh{i}", (128, 8, 16), BF16) for i in range(2)]
        self.XB2 = sb("xb2", (128, 8, 512), BF16)
        self.ZFT = sb("zft", (128, 4, 256), BF16)
        self.ZV = sb("zv", (128, 4, 256), F32)
        self.GSQ = sb("gsq", (128, 512), F32)
        self.GTH = sb("gth", (128, 512), F32)
        self.VN = sb("vn", (128, 4, 256), BF16)
        self.ST6 = sb("st6", (128, 4, 6), F32)
        self.MV = sb("mv", (128, 4, 2), F32)
        self.RS = sb("rs", (128, 4, 1), F32)
        self.TMPS = sb("tmps", (128, 2, 128), F32)
        self.IDENT = sb("identf", (128, 128), F32)
        self.IDB = sb("identb", (128, 128), BF16)
        self.ONES = sb("ones", (128, 128), F32)
        self.MHALF = sb("mhalf", (128, 8), F32)
        self.EPSC = sb("epsc", (128, 1), F32)
        self.DUM = sb("dum", (128, 2), F32)
        self.NMR = sb("nmr", (128, 512), F32)
        self.STGA = sb("stga", (128, 128), F32)
        self.STGB = sb("stgb", (128, 128), F32)
        self.STGC = sb("stgc", (128, 128), F32)
        self.STGD = sb("stgd", (128, 2, 256), F32)
        self.STGE = sb("stge", (128, 256), F32)
        self.LNGB = sb("lngb", (128, 96), F32)
        self.PVB = sb("pvb", (128, 80), F32)
        self.HBG = sb("hbg", (128, 64), F32)
        self.SGG = sb("sgg", (128, 2, 256), F32)
        self.SGB = sb("sgb", (128, 2, 256), F32)
        self.BS = sb("bs", (128, 2, 2, 128), F32)
        self.SGW = sb("sgw", (128, 2, 4, 128), BF16)
        self.PWF = sb("pwf", (128, 2, 2, 128), F32)
        self.PW = sb("pw", (128, 2, 2, 128), BF16)
        self.T2B = {4096: sb("t2b4096", (128, 2, 256), BF16), 2048: sb("t2b2048", (128, 2, 256), BF16)}
        self.BDCB = sb("bdcb", (128, 128), BF16)
        self.BDSB = sb("bdsb", (128, 128), BF16)
        self.POOLC = sb("poolc_sb", (128, 2, 2, 8), F32)
        self.PS = [nc.alloc_psum_tensor(f"ps{i}", [128, 512], F32) for i in range(8)]
        self.XBs = [self.XBa, self.XB2[:]]
        self.par = 0

    def mm(self, bank, out, lhsT, rhs, start, stop, reads, guards=()):
        self.S.op("pe", lambda e: e.matmul(out, lhsT, rhs, start=start, stop=stop),
                  reads=reads, writes=[("PS", bank)], guards=guards)

    def wload(self, src, E, key):
        s = self.ring_i % NSLOT
        self.ring_i += 1
        dst = self.RING[s][:, 0:E]
        self.S.dma("sp", lambda e: e.dma_start(out=dst, in_=src), reads=[key], writes=[("RING", s)])
        return s

    def ringv(self, s, k, n):
        return self.RING[s][:, 0:k * n].rearrange("p (k n) -> p k n", k=k)

    def prologue(self):
        S = self.S
        nc = self.nc

        def cast(dst, src, key):
            S.dma("pool", lambda e: e.dma_start(out=dst, in_=src), writes=[key])

        def cast_kn(ws, src2d, nunits, key):
            sv = src2d.rearrange("(k p) n -> p k n", p=128)
            for u in range(nunits):
                cast(ws[u].rearrange("p (k n) -> p k n", k=8), sv[:, :, u * 256:(u + 1) * 256], key + (u,))

        def cast_w2(l, f):
            sv = self.ffn_w2[l, f].rearrange("(h p) n -> p h n", p=128)
            for oc in range(8):
                for hf in range(2):
                    cast(self.ws_w2[l, f, oc * 2 + hf].rearrange("p (h n) -> p h n", h=11),
                         sv[:, hf * 11:(hf + 1) * 11, oc * 128:(oc + 1) * 128], ("WS", "w2", l, f, oc * 2 + hf))

        def cast_mix(l):
            cast_kn(self.ws_win[l], self.w_in[l], 7, ("WS", "win", l))
            for i in range(4):
                cast_kn(self.ws_wg[l, i], self.w_gate[l, i], 4, ("WS", "wg", l, i))
                cast(self.ws_wbr[l, i].rearrange("p (j n) -> p j n", j=2),
                     self.w_branch[l, i].rearrange("(j p) n -> p j n", p=128), ("WS", "wbr", l, i))
            cast_kn(self.ws_wo[l], self.w_out[l], 4, ("WS", "wo", l))

        def cast_ffn(l, f):
            cast_kn(self.ws_w1[l, f], self.ffn_w1[l, f], 11, ("WS", "w1", l, f))
            cast_kn(self.ws_w3[l, f], self.ffn_w3[l, f], 11, ("WS", "w3", l, f))
            cast_w2(l, f)

        cast_ffn(0, 0)
        cast(self.ws_win[0, 3].rearrange("p (k n) -> p k n", k=8),
             self.w_in[0].rearrange("(k p) n -> p k n", p=128)[:, :, 768:1024], ("WS", "win", 0, 3))
        for Sq in (4096, 2048):
            cast(self.ws_t1[Sq], self.c_t1[Sq], ("WS", "t1", Sq))
        self._save = S.ops
        late = []
        S.ops = late
        sv0 = self.w_in[0].rearrange("(k p) n -> p k n", p=128)
        for u in (0, 1, 2, 4, 5, 6):
            cast(self.ws_win[0, u].rearrange("p (k n) -> p k n", k=8), sv0[:, :, u * 256:(u + 1) * 256],
                 ("WS", "win", 0, u))
        for i in range(4):
            cast_kn(self.ws_wg[0, i], self.w_gate[0, i], 4, ("WS", "wg", 0, i))
            cast(self.ws_wbr[0, i].rearrange("p (j n) -> p j n", j=2),
                 self.w_branch[0, i].rearrange("(j p) n -> p j n", p=128), ("WS", "wbr", 0, i))
        cast_kn(self.ws_wo[0], self.w_out[0], 4, ("WS", "wo", 0))
        cast_ffn(0, 1)
        cast_ffn(1, 0)
        cast(self.ws_win[1, 3].rearrange("p (k n) -> p k n", k=8),
             self.w_in[1].rearrange("(k p) n -> p k n", p=128)[:, :, 768:1024], ("WS", "win", 1, 3))
        late2 = []
        S.ops = late2
        sv1 = self.w_in[1].rearrange("(k p) n -> p k n", p=128)
        for u in (0, 1, 2, 4, 5, 6):
            cast(self.ws_win[1, u].rearrange("p (k n) -> p k n", k=8), sv1[:, :, u * 256:(u + 1) * 256],
                 ("WS", "win", 1, u))
        for i in range(4):
            cast_kn(self.ws_wg[1, i], self.w_gate[1, i], 4, ("WS", "wg", 1, i))
            cast(self.ws_wbr[1, i].rearrange("p (j n) -> p j n", j=2),
                 self.w_branch[1, i].rearrange("(j p) n -> p j n", p=128), ("WS", "wbr", 1, i))
        cast_kn(self.ws_wo[1], self.w_out[1], 4, ("WS", "wo", 1))
        cast_ffn(1, 1)
        S.ops = self._save
        self.late_casts = late
        self.late_casts2 = late2

        S.dma("sp", lambda e: e.dma_start(out=self.IDENT[:], in_=self.c_ident), writes=["IDENT"])
        S.op("dve", lambda e: e.tensor_copy(out=self.IDB[:], in_=self.IDENT[:]), reads=["IDENT"], writes=["IDB"])
        S.op("dve", lambda e: e.memset(self.ONES[:], 1.0), writes=["ONES"])
        S.op("dve", lambda e: e.memset(self.ONESB[:], 1.0), writes=["ONESB"])
        S.op("dve", lambda e: e.memset(self.MHALF[:], -0.5), writes=["MHALF"])
        S.op("dve", lambda e: e.memset(self.EPSC[:], EPS1), writes=["EPSC"])
        S.op("dve", lambda e: e.memset(self.DUM[:], 1.0), writes=["DUM"])
        S.dma("sp", lambda e: e.dma_start(out=self.STGA[0:48, :],
                                          in_=self.ln_g.rearrange("l i (k p) -> (l i k) p", p=128)), writes=["STGA0"])
        S.dma("sp", lambda e: e.dma_start(out=self.STGA[48:96, :],
                                          in_=self.ln_b.rearrange("l i (k p) -> (l i k) p", p=128)), writes=["STGA1"])
        bk = self.pbank()
        S.op("pe", lambda e: e.transpose(self.PS[bk][:, 0:96], self.STGA[0:96, :], self.IDENT[0:96, 0:96]),
             reads=["STGA0", "STGA1", "IDENT"], writes=[("PS", bk)])
        S.op("dve", lambda e, bk=bk: e.tensor_copy(out=self.LNGB[:], in_=self.PS[bk][:, 0:96]),
             reads=[("PS", bk)], writes=["LNGB"])
        S.dma("sp", lambda e: e.dma_start(out=self.STGB[0:64, :],
                                          in_=self.b_gate.rearrange("l i (k p) -> (l i k) p", p=128)),
              writes=["STGB0"])
        S.dma("sp", lambda e: e.dma_start(out=self.STGB[64:76, :],
                                          in_=self.conv_w.rearrange("l t (j p) -> (l t j) p", p=128)), writes=["STGB1"])
        S.dma("sp", lambda e: e.dma_start(out=self.STGB[76:80, :],
                                          in_=self.pool_scale.rearrange("l (j p) -> (l j) p", p=128)), writes=["STGB2"])
        bk2 = self.pbank()
        S.op("pe", lambda e: e.transpose(self.PS[bk2][:, 0:80], self.STGB[0:80, :], self.IDENT[0:80, 0:80]),
             reads=["STGB0", "STGB1", "STGB2", "IDENT"], writes=[("PS", bk2)])
        S.op("dve", lambda e: e.tensor_copy(out=self.PVB[:], in_=self.PS[bk2][:, 0:80]),
             reads=[("PS", bk2)], writes=["PVB"])
        S.op("act", lambda e: e.mul(out=self.HBG[:], in_=self.PVB[:, 0:64], mul=0.5), reads=["PVB"], writes=["HBG"])
        for l in range(2):
            S.dma("sp", lambda e, l=l: e.dma_start(out=self.SGG[:, l, :],
                                                   in_=self.sgu_ln_g[l:l + 1, :].partition_broadcast(128)),
                  writes=["SGG"])
            S.dma("sp", lambda e, l=l: e.dma_start(out=self.SGB[:, l, :],
                                                   in_=self.sgu_ln_b[l:l + 1, :].partition_broadcast(128)),
                  writes=["SGB"])
            for j in range(2):
                for hh in range(2):
                    S.dma("sp", lambda e, l=l, j=j, hh=hh: e.dma_start(
                        out=self.BS[hh * 64:(hh + 1) * 64, l, j, :],
                        in_=self.sgu_b[l, 2 * j + hh:2 * j + hh + 1, :].partition_broadcast(64)), writes=["BS"])
        for l in range(2):
            for h in range(4):
                S.dma("sp", lambda e, l=l, h=h: e.dma_start(out=self.STGC[:], in_=self.sgu_w[l, h]),
                      writes=["STGC"])
                bk3 = self.pbank()
                S.op("pe", lambda e, bk3=bk3: e.transpose(self.PS[bk3][:, 0:128], self.STGC[:], self.IDENT[:]),
                     reads=["STGC", "IDENT"], writes=[("PS", bk3)])
                S.op("act", lambda e, bk3=bk3, l=l, h=h: e.copy(out=self.SGW[:, l, h, :], in_=self.PS[bk3][:, 0:128]),
                     reads=[("PS", bk3)], writes=["SGW"])
        S.op("dve", lambda e: e.memset(self.PWF[:], 0.0), writes=["PWF"])
        for l in range(2):
            for j in range(2):
                for gg in range(2):
                    S.dma("sp", lambda e, l=l, j=j, gg=gg: e.dma_start(
                        out=self.PWF[gg * 64:(gg + 1) * 64, l, j, gg * 64:(gg + 1) * 64],
                        in_=self.pool_w[l, 2 * j + gg]), reads=["PWF"], writes=[("PWFd", l, j, gg)])
        S.op("dve", lambda e: e.tensor_copy(out=self.PW[:], in_=self.PWF[:]),
             reads=["PWF"] + [("PWFd", l, j, gg) for l in range(2) for j in range(2) for gg in range(2)],
             writes=["PW"])
        for Sq in (4096, 2048):
            S.dma("sp", lambda e, Sq=Sq: e.dma_start(out=self.STGD[:], in_=self.c_t2[Sq].rearrange("r p n -> p r n")),
                  writes=["STGD"])
            S.op("dve", lambda e, Sq=Sq: e.tensor_copy(out=self.T2B[Sq][:], in_=self.STGD[:]),
                 reads=["STGD"], writes=[("T2B", Sq)])
        S.dma("sp", lambda e: e.dma_start(out=self.STGE[:, 0:128], in_=self.c_bdc), writes=["STGE0"])
        S.dma("sp", lambda e: e.dma_start(out=self.STGE[:, 128:256], in_=self.c_bds), writes=["STGE1"])
        S.op("dve", lambda e: e.tensor_copy(out=self.BDCB[:], in_=self.STGE[:, 0:128]), reads=["STGE0"], writes=["BDCB"])
        S.op("dve", lambda e: e.tensor_copy(out=self.BDSB[:], in_=self.STGE[:, 128:256]), reads=["STGE1"], writes=["BDSB"])
        S.dma("sp", lambda e: e.dma_start(out=self.POOLC[:].rearrange("p a b c -> p (a b c)"), in_=self.c_poolc),
              writes=["POOLC"])

    def ffn(self, l, f, ln, after=None):
        S = self.S
        X, XB, G, PS = self.X, self.XBs[self.par], self.G, self.PS
        for u in range(11):
            s1 = self.wload(self.ws_w1[l, f, u], 2048, ("WS", "w1", l, f, u))
            s3 = self.wload(self.ws_w3[l, f, u], 2048, ("WS", "w3", l, f, u))
            W1 = self.ringv(s1, 8, 256)
            W3 = self.ringv(s3, 8, 256)
            for h2 in range(2):
                hc = 2 * u + h2
                ba = self.pbank()
                bb = self.pbank()
                for kc in range(8):
                    self.mm(ba, PS[ba][:], W1[:, kc, h2 * 128:(h2 + 1) * 128], XB[:, kc, :], kc == 0, kc == 7,
                            [("RING", s1), ("XB", self.par, kc)])
                for kc in range(8):
                    self.mm(bb, PS[bb][:], W3[:, kc, h2 * 128:(h2 + 1) * 128], XB[:, kc, :], kc == 0, kc == 7,
                            [("RING", s3), ("XB", self.par, kc)])
                t = self.rr("sil", 2)
                S.op("act", lambda e, t=t, ba=ba: e.activation(out=self.SIL[t][:], in_=PS[ba][:], func=AF.Silu),
                     reads=[("PS", ba)], writes=[("SIL", t)])
                S.op("dve", lambda e, t=t, bb=bb, hc=hc: e.tensor_tensor(out=G[:, hc, :], in0=PS[bb][:],
                                                                         in1=self.SIL[t][:], op=ALU.mult),
                     reads=[("PS", bb), ("SIL", t)], writes=[("G", hc), "AL"])
        S.op("act", lambda e: e.activation(out=self.DUM[:, 1:2], in_=self.DUM[:, 0:1], func=AF.Ln),
             reads=["DUM"], writes=["DUM1"])
        for oc in range(8):
            s2a = self.wload(self.ws_w2[l, f, oc * 2], 1408, ("WS", "w2", l, f, oc * 2))
            s2b = self.wload(self.ws_w2[l, f, oc * 2 + 1], 1408, ("WS", "w2", l, f, oc * 2 + 1))
            W2a = self.ringv(s2a, 11, 128)
            W2b = self.ringv(s2b, 11, 128)
            bc = self.pbank()
            for hc in range(22):
                W2, s2 = (W2a, s2a) if hc < 11 else (W2b, s2b)
                self.mm(bc, PS[bc][:], W2[:, hc % 11, :], G[:, hc, :], hc == 0, hc == 21,
                        [("RING", s2), ("G", hc)], guards=["AL"])
            S.op("dve", lambda e, bc=bc, oc=oc: e.scalar_tensor_tensor(
                out=X[:, oc, :], in0=PS[bc][:], scalar=RES_SCALE, in1=X[:, oc, :], op0=ALU.mult, op1=ALU.add),
                 reads=[("PS", bc), ("X", self.xpar, oc)], writes=[("X", self.xpar, oc)])
            if oc >= 1:
                self.ln_chunk(oc - 1)
        self.ln_chunk(7)
        self.ln_finish(*ln)
        if after is not None:
            after()

    def ln_chunk(self, kc):
        S = self.S
        X, PS = self.X, self.PS
        q = self.rr("sq", 2)
        S.op("act", lambda e, q=q, kc=kc: e.copy(out=self.YB[q][:], in_=X[:, kc, :]),
             reads=[("X", self.xpar, kc)], writes=[("YB", q)])
        S.op("act", lambda e, q=q, kc=kc: e.activation(out=self.SQ[q][:], in_=X[:, kc, :], func=AF.Square),
             reads=[("X", self.xpar, kc)], writes=[("SQ", q)])
        self.mm(6, PS[6][:], self.ONESB[:], self.YB[q][:], kc == 0, kc == 7, ["ONESB", ("YB", q)])
        self.mm(7, PS[7][:], self.ONESB[:], self.SQ[q][:], kc == 0, kc == 7, ["ONESB", ("SQ", q)])

    def ln_finish(self, l, i):
        S = self.S
        X, XB, PS = self.X, self.XBs[self.par], self.PS
        S.op("act", lambda e: e.activation(out=self.VAR[:], in_=PS[6][:], func=AF.Square),
             reads=[("PS", 6)], writes=["VAR"])
        S.op("dve", lambda e: e.scalar_tensor_tensor(out=self.VAR[:], in0=PS[7][:], scalar=float(D),
                                                     in1=self.VAR[:], op0=ALU.mult, op1=ALU.subtract),
             reads=[("PS", 7), "VAR"], writes=["VAR"])
        S.op("act", lambda e: e.activation(out=self.RSTD[:], in_=self.VAR[:], func=AF.Ln, bias=self.EPSC[:, 0:1],
                                           scale=1.0 / (D * D)), reads=["VAR", "EPSC"], writes=["RSTD"])
        S.op("act", lambda e: e.activation(out=self.RSTD[:], in_=self.RSTD[:], func=AF.Exp, scale=-0.5),
             reads=["RSTD"], writes=["RSTD"])
        S.op("dve", lambda e: e.scalar_tensor_tensor(out=self.NMR[:], in0=PS[6][:], scalar=-1.0 / D, in1=self.RSTD[:],
                                                     op0=ALU.mult, op1=ALU.mult),
             reads=[("PS", 6), "RSTD"], writes=["NMR"])
        col = (l * 3 + i) * 8
        par = self.par
        for kc in range(8):
            gp = self.LNGB[:, col + kc:col + kc + 1]
            bp = self.LNGB[:, 48 + col + kc:48 + col + kc + 1]
            S.op("dve", lambda e, kc=kc: e.tensor_tensor(out=X[:, kc, :], in0=X[:, kc, :], in1=self.RSTD[:],
                                                         op=ALU.mult),
                 reads=[("X", self.xpar, kc), "RSTD"], writes=[("X", self.xpar, kc)])
            S.op("dve" if kc % 2 == 0 else "pool",
                 lambda e, kc=kc: e.tensor_tensor(out=X[:, kc, :], in0=X[:, kc, :], in1=self.NMR[:], op=ALU.add),
                 reads=[("X", self.xpar, kc), "NMR"], writes=[("X", self.xpar, kc)])
            S.op("act", lambda e, kc=kc, gp=gp, bp=bp: e.activation(out=XB[:, kc, :], in_=X[:, kc, :],
                                                                     func=AF.Identity, bias=bp, scale=gp),
                 reads=[("X", self.xpar, kc), "LNGB"], writes=[("XB", par, kc)])
        for kc in range(8):
            gp = self.LNGB[:, col + kc:col + kc + 1]
            bp = self.LNGB[:, 48 + col + kc:48 + col + kc + 1]
            if kc % 2 == 0:
                S.op("act", lambda e, kc=kc, gp=gp, bp=bp: e.activation(out=X[:, kc, :], in_=X[:, kc, :],
                                                                         func=AF.Identity, bias=bp, scale=gp),
                     reads=[("X", self.xpar, kc), "LNGB"], writes=[("X", self.xpar, kc)])
            else:
                S.op("dve", lambda e, kc=kc, gp=gp, bp=bp: e.tensor_scalar(out=X[:, kc, :], in0=X[:, kc, :],
                                                                            scalar1=gp, scalar2=bp,
                                                                            op0=ALU.mult, op1=ALU.add),
                     reads=[("X", self.xpar, kc), "LNGB"], writes=[("X", self.xpar, kc)])

    def zf_and_spill(self, l, b):
        S = self.S
        PS, XB = self.PS, self.XBs[self.par]
        tok0 = b * TB
        s = self.wload(self.ws_win[l, 3], 2048, ("WS", "win", l, 3))
        Wf = self.ringv(s, 8, 256)
        for c4 in range(4):
            if c4 % 2 == 0:
                bk = self.pbank()
            co = (c4 % 2) * 256
            for kc in range(8):
                self.mm(bk, PS[bk][:, co:co + 256], XB[:, kc, c4 * 128:(c4 + 1) * 128], Wf[:, kc, :],
                        kc == 0, kc == 7, [("RING", s), ("XB", self.par, kc)])
            if c4 % 2 == 1:
                S.op("act", lambda e, bk=bk, c4=c4: e.copy(out=self.ZFT[:, c4 - 1:c4 + 1, :],
                                                          in_=PS[bk][:].rearrange("p (c n) -> p c n", c=2)),
                     reads=[("PS", bk)], writes=[("ZFT", c4 // 2)])
        S.dma("act", lambda e: e.dma_start(out=self.zfs[l][tok0:tok0 + TB, :].rearrange("(c p) f -> p c f", p=128),
                                           in_=self.ZFT[:]),
              reads=[("ZFT", 0), ("ZFT", 1)], writes=[("ZFS", l, b)])
        S.dma("act", lambda e: e.dma_start(out=self.xsb[l].rearrange("k p n -> p k n")[:, :, tok0:tok0 + TB],
                                           in_=XB[:]),
              reads=[("XB", self.par, kc) for kc in range(8)], writes=[("XSB", l, b)])
        Xl = self.X
        S.dma("act", lambda e: e.dma_start(out=self.xs[l].rearrange("k p n -> p k n")[:, :, tok0:tok0 + TB],
                                           in_=Xl[:]),
              reads=[("X", self.xpar, kc) for kc in range(8)], writes=[("XS", l, b)])

    def load_block(self, l, b, first, last):
        S = self.S
        tok0 = b * TB
        xsv = self.xs[l].rearrange("k p n -> p k n")
        xbv = self.xsb[l].rearrange("k p n -> p k n")
        XB = self.XBs[self.par]
        XBH = self.XBHs[self.par]
        S.dma("sp", lambda e: e.dma_start(out=XB[:], in_=xbv[:, :, tok0:tok0 + TB]),
              reads=[("XSB", l, b)], writes=[("XB", self.par, kc) for kc in range(8)])
        if first:
            S.op("pool", lambda e: e.memset(XBH[:, :, 0:8], 0.0), writes=[("XBH", self.par, 0)])
        else:
            S.dma("sp", lambda e: e.dma_start(out=XBH[:, :, 0:8], in_=xbv[:, :, tok0 - 8:tok0]),
                  reads=[("XSB", l, b - 1)], writes=[("XBH", self.par, 0)])
        if last:
            S.op("pool", lambda e: e.memset(XBH[:, :, 8:16], 0.0), writes=[("XBH", self.par, 1)])
        else:
            S.dma("sp", lambda e: e.dma_start(out=XBH[:, :, 8:16], in_=xbv[:, :, tok0 + TB:tok0 + TB + 8]),
                  reads=[("XSB", l, b + 1)], writes=[("XBH", self.par, 1)])

    def load_block_late(self, l, b, si):
        S = self.S
        tok0 = b * TB
        xsv = self.xs[l].rearrange("k p n -> p k n")
        S.dma("sp", lambda e: e.dma_start(out=self.BRF[:], in_=self.brfs[l].rearrange("j p n -> p j n")[:, :, tok0:tok0 + TB]),
              reads=[("BRFS", l, si, 0), ("BRFS", l, si, 1)], writes=["BRF"])
        Xl = self.X
        S.dma("sp", lambda e: e.dma_start(out=Xl[:], in_=xsv[:, :, tok0:tok0 + TB]),
              reads=[("XS", l, b)], writes=[("X", self.xpar, kc) for kc in range(8)])

    def gelu2(self, zfull, zkey, n):
        S = self.S
        for h0 in range(0, n, 512):
            zbuf = zfull[:, h0:h0 + 512]
            sq = self.GSQ[:, 0:512]
            th = self.GTH[:, 0:512]
            S.op("act", lambda e, zbuf=zbuf, sq=sq: e.activation(out=sq, in_=zbuf, func=AF.Square), reads=[zkey],
                 writes=["GSQ"])
            S.op("dve", lambda e, sq=sq: e.tensor_scalar(out=sq, in0=sq, scalar1=0.044715, scalar2=1.0, op0=ALU.mult,
                                                          op1=ALU.add), reads=["GSQ"], writes=["GSQ"])
            S.op("dve", lambda e, zbuf=zbuf, sq=sq: e.tensor_tensor(out=sq, in0=sq, in1=zbuf, op=ALU.mult),
                 reads=["GSQ", zkey], writes=["GSQ"])
            S.op("act", lambda e, sq=sq, th=th: e.activation(out=th, in_=sq, func=AF.Tanh, scale=GELU_C),
                 reads=["GSQ"], writes=["GTH"])
            S.op("dve", lambda e, zbuf=zbuf, th=th: e.scalar_tensor_tensor(out=zbuf, in0=th, scalar=1.0, in1=zbuf,
                                                                           op0=ALU.add, op1=ALU.mult),
                 reads=["GTH", zkey], writes=[zkey])

    def mix_part1(self, l, par, units):
        S = self.S
        PS = self.PS
        XB = self.XBs[par]
        XBH = self.XBHs[par]
        dest = {0: self.H, 2: self.GC, 6: self.ZP}
        dkey = {0: "H", 1: "GB", 2: "GC", 4: "U", 6: "ZP"}
        for u in units:
            if u == 5:
                continue
            s = self.wload(self.ws_win[l, u], 2048, ("WS", "win", l, u))
            W = self.ringv(s, 8, 256)
            for j in range(2):
                bk = self.pbank()
                for kc in range(8):
                    self.mm(bk, PS[bk][:], W[:, kc, j * 128:(j + 1) * 128], XB[:, kc, :], kc == 0, kc == 7,
                            [("RING", s), ("XB", par, kc)])
                if u in dest:
                    o = dest[u][:, j, 8:520]
                elif u == 1:
                    o = self.GB[:, j, :]
                else:
                    o = self.U[:, j, :]
                S.op("act", lambda e, o=o, bk=bk: e.copy(out=o, in_=PS[bk][:]), reads=[("PS", bk)],
                     writes=["Uall" if u == 4 else (dkey[u], j)])
            if u in dest:
                bk = self.pbank()
                for j in range(2):
                    for kc in range(8):
                        self.mm(bk, PS[bk][:, j * 16:(j + 1) * 16], W[:, kc, j * 128:(j + 1) * 128], XBH[:, kc, :],
                                kc == 0, kc == 7, [("RING", s), ("XBH", par, 0), ("XBH", par, 1)])
                for j in range(2):
                    S.op("act", lambda e, u=u, j=j, bk=bk: e.copy(out=dest[u][:, j, 0:8], in_=PS[bk][:, j * 16:j * 16 + 8]),
                         reads=[("PS", bk)], writes=[(dkey[u], j)])
                    S.op("act", lambda e, u=u, j=j, bk=bk: e.copy(out=dest[u][:, j, 520:528],
                                                                  in_=PS[bk][:, j * 16 + 8:j * 16 + 16]),
                         reads=[("PS", bk)], writes=[(dkey[u], j)])
        if 5 in units:
            s = self.wload(self.ws_win[l, 5], 2048, ("WS", "win", l, 5))
            Wv = self.ringv(s, 8, 256)
            for c4 in range(4):
                if c4 % 2 == 0:
                    bk = self.pbank()
                co = (c4 % 2) * 256
                for kc in range(8):
                    self.mm(bk, PS[bk][:, co:co + 256], XB[:, kc, c4 * 128:(c4 + 1) * 128], Wv[:, kc, :],
                            kc == 0, kc == 7, [("RING", s), ("XB", par, kc)])
                if c4 % 2 == 1:
                    S.op("act", lambda e, bk=bk, c4=c4: e.copy(out=self.ZV[:, c4 - 1:c4 + 1, :],
                                                              in_=PS[bk][:].rearrange("p (c n) -> p c n", c=2)),
                         reads=[("PS", bk)], writes=["ZV"])

    def mix(self, l, b, si, first, last, after=None):
        S = self.S
        PS, X, XB = self.PS, self.X, self.XBs[self.par]
        tok0 = b * TB
        self.load_block_late(l, b, si)
        for j in range(2):
            S.op("pool", lambda e, j=j: e.tensor_tensor(out=self.GC[:, j, :], in0=self.GC[:, j, :], in1=self.H[:, j, :],
                                                        op=ALU.mult),
                 reads=[("GC", j), ("H", j)], writes=[("GC", j)])
            cw = [self.PVB[:, 64 + (l * 3 + t) * 2 + j:64 + (l * 3 + t) * 2 + j + 1] for t in range(3)]
            S.op("dve", lambda e, j=j, cw=cw: e.tensor_scalar(out=self.ACC[:, j, :], in0=self.GC[:, j, 7:519],
                                                              scalar1=cw[0], scalar2=None, op0=ALU.mult),
                 reads=[("GC", j), "PVB"], writes=[("ACC", j), "AL"])
            S.op("dve", lambda e, j=j, cw=cw: e.scalar_tensor_tensor(out=self.ACC[:, j, :], in0=self.GC[:, j, 8:520],
                                                                     scalar=cw[1], in1=self.ACC[:, j, :],
                                                                     op0=ALU.mult, op1=ALU.add),
                 reads=[("GC", j), ("ACC", j), "PVB"], writes=[("ACC", j), "AL"])
            S.op("dve", lambda e, j=j, cw=cw: e.scalar_tensor_tensor(out=self.ACC[:, j, :], in0=self.GC[:, j, 9:521],
                                                                     scalar=cw[2], in1=self.ACC[:, j, :],
                                                                     op0=ALU.mult, op1=ALU.add),
                 reads=[("GC", j), ("ACC", j), "PVB"], writes=[("ACC", j), "AL"])
            S.op("pool", lambda e, j=j: e.tensor_tensor(out=self.BRA[:, j, :], in0=self.GB[:, j, :],
                                                        in1=self.ACC[:, j, :], op=ALU.mult),
                 reads=[("GB", j), ("ACC", j)], writes=[("BRA", j)], guards=["AL"])

        ZP, PA, PB, PC, PD = self.ZP, self.PA, self.PB, self.PC, self.PD
        S.op("pool", lambda e: e.tensor_tensor(out=PA[:, :, 1:528], in0=ZP[:, :, 0:527], in1=ZP[:, :, 1:528], op=ALU.add),
             reads=[("ZP", 0), ("ZP", 1)], writes=["PA"])
        S.op("pool", lambda e: e.tensor_tensor(out=PB[64:128, 0, 2:527], in0=PA[64:128, 0, 1:526],
                                               in1=PA[64:128, 0, 3:528], op=ALU.add), reads=["PA"], writes=["PB0"])
        S.op("pool", lambda e: e.tensor_tensor(out=PB[:, 1, 2:527], in0=PA[:, 1, 1:526], in1=PA[:, 1, 3:528],
                                               op=ALU.add), reads=["PA"], writes=["PB1"])
        S.op("pool", lambda e: e.tensor_tensor(out=PC[:, 0, 4:525], in0=PB[:, 1, 2:523], in1=PB[:, 1, 6:527],
                                               op=ALU.add), reads=["PB1"], writes=["PC"])
        S.op("pool", lambda e: e.tensor_tensor(out=PD[64:128, 0, 8:521], in0=PC[64:128, 0, 4:517],
                                               in1=PC[64:128, 0, 12:525], op=ALU.add), reads=["PC"], writes=["PD"])
        tots = [(0, 0, 64, PA[0:64, 0, :], "PA", 2.0), (0, 64, 128, PB[64:128, 0, :], "PB0", 4.0),
                (1, 0, 64, PC[0:64, 0, :], "PC", 8.0), (1, 64, 128, PD[64:128, 0, :], "PD", 16.0)]
        for (j, p0, p1, tot, key, win) in tots:
            if first:
                S.op("pool", lambda e, tot=tot, j=j, p0=p0, p1=p1: e.tensor_tensor(
                    out=tot[:, 8:16], in0=tot[:, 8:16], in1=self.POOLC[p0:p1, j, 0, :], op=ALU.mult),
                     reads=[key, "POOLC"], writes=[key])
            if last:
                S.op("pool", lambda e, tot=tot, j=j, p0=p0, p1=p1: e.tensor_tensor(
                    out=tot[:, 512:520], in0=tot[:, 512:520], in1=self.POOLC[p0:p1, j, 1, :], op=ALU.mult),
                     reads=[key, "POOLC"], writes=[key])
            S.op("dve", lambda e, tot=tot, j=j, p0=p0, p1=p1, win=win: e.scalar_tensor_tensor(
                out=self.PL[p0:p1, j, :], in0=tot[:, 8:520], scalar=1.0 / win, in1=ZP[p0:p1, j, 8:520],
                op0=ALU.mult, op1=ALU.subtract),
                 reads=[key, ("ZP", j)], writes=[("PL", j, p0), "AL"])
        for j in range(2):
            bk = self.pbank()
            self.mm(bk, PS[bk][:], self.PW[:, l, j, :], self.PL[:, j, :], True, True,
                    ["PW", ("PL", j, 0), ("PL", j, 64)], guards=["AL"])
            psc = self.PVB[:, 76 + l * 2 + j:76 + l * 2 + j + 1]
            S.op("act", lambda e, bk=bk, j=j, psc=psc: e.activation(out=self.BRD[:, j, :], in_=PS[bk][:],
                                                                    func=AF.Copy, scale=psc),
                 reads=[("PS", bk), "PVB"], writes=[("BRD", j)])

        self.gelu2(self.U[:].rearrange("p j n -> p (j n)"), "Uall", 1024)
        self.gelu2(self.ZV[:].rearrange("p c n -> p (c n)"), "ZV", 1024)
        for c4 in range(4):
            S.op("dve", lambda e, c4=c4: e.bn_stats(out=self.ST6[:, c4, :], in_=self.ZV[:, c4, :]),
                 reads=["ZV"], writes=["ST6"])
            S.op("dve", lambda e, c4=c4: e.bn_aggr(out=self.MV[:, c4, :], in_=self.ST6[:, c4, :]),
                 reads=["ST6"], writes=["MV"])
        S.op("dve", lambda e: e.tensor_scalar(out=self.RS[:], in0=self.MV[:, :, 1:2], scalar1=4.0 * LN_EPS, scalar2=None,
                                              op0=ALU.add), reads=["MV"], writes=["RS"])
        S.op("pool", lambda e: e.tensor_tensor(out=self.RS[:], in0=self.RS[:], in1=self.MHALF[:, 0:4].rearrange("p (a b) -> p a b", b=1),
                                               op=ALU.pow), reads=["RS", "MHALF"], writes=["RS"])
        for c4 in range(4):
            S.op("dve", lambda e, c4=c4: e.tensor_scalar(out=self.ZV[:, c4, :], in0=self.ZV[:, c4, :],
                                                         scalar1=self.MV[:, c4, 0:1], scalar2=self.RS[:, c4, :],
                                                         op0=ALU.subtract, op1=ALU.mult),
                 reads=["ZV", "MV", "RS"], writes=["ZV"])
            S.op("pool", lambda e, c4=c4: e.tensor_tensor(out=self.ZV[:, c4, :], in0=self.ZV[:, c4, :],
                                                          in1=self.SGG[:, l, :], op=ALU.mult),
                 reads=["ZV", "SGG"], writes=["ZV"])
            S.op("pool", lambda e, c4=c4: e.tensor_tensor(out=self.VN[:, c4, :], in0=self.ZV[:, c4, :],
                                                          in1=self.SGB[:, l, :], op=ALU.add),
                 reads=["ZV", "SGB"], writes=[("VN", c4)])

        brs = [(1, self.BRF, ["BRF"], []), (0, self.BRA, [("BRA", 0), ("BRA", 1)], []),
               (3, self.BRD, [("BRD", 0), ("BRD", 1)], []), (2, self.BRC, [("BRC", 0), ("BRC", 1)], [])]
        for n_i, (i, BR, brkeys, _) in enumerate(brs):
            if i == 2:
                for c4 in range(4):
                    bk = self.pbank()
                    for j in range(2):
                        for hh in range(2):
                            r = j * 2 + hh
                            self.mm(bk, PS[bk][:, r * 128:(r + 1) * 128], self.VN[:, c4, j * 128:(j + 1) * 128],
                                    self.SGW[:, l, 2 * j + hh, :], True, True, [("VN", c4), "SGW"])
                    for j in range(2):
                        for hh in range(2):
                            r = j * 2 + hh
                            S.op("dve", lambda e, bk=bk, j=j, hh=hh, r=r: e.tensor_tensor(
                                out=self.TMPS[hh * 64:(hh + 1) * 64, j, :],
                                in0=PS[bk][hh * 64:(hh + 1) * 64, r * 128:(r + 1) * 128],
                                in1=self.BS[hh * 64:(hh + 1) * 64, l, j, :], op=ALU.add),
                                 reads=[("PS", bk), "BS"], writes=[("TMPS", j)])
                        S.op("dve", lambda e, j=j, c4=c4: e.scalar_tensor_tensor(
                            out=self.BRC[:, j, c4 * 128:(c4 + 1) * 128], in0=self.TMPS[:, j, :], scalar=0.5,
                            in1=self.U[:, j, c4 * 128:(c4 + 1) * 128], op0=ALU.mult, op1=ALU.mult),
                             reads=[("TMPS", j), "Uall"], writes=[("BRC", j)])
            sbr = self.wload(self.ws_wbr[l, i], 2048, ("WS", "wbr", l, i))
            WBR = self.ringv(sbr, 2, 1024)
            for q in range(4):
                sg = self.wload(self.ws_wg[l, i, q], 2048, ("WS", "wg", l, i, q))
                WG = self.ringv(sg, 8, 256)
                for o2 in range(2):
                    oc = 2 * q + o2
                    bg = self.pbank()
                    bp = self.pbank()
                    for kc in range(8):
                        self.mm(bg, PS[bg][:], WG[:, kc, o2 * 128:(o2 + 1) * 128], XB[:, kc, :], kc == 0, kc == 7,
                                [("RING", sg), ("XB", self.par, kc)])
                    for j in range(2):
                        self.mm(bp, PS[bp][:], WBR[:, j, oc * 128:(oc + 1) * 128], BR[:, j, :], j == 0, j == 1,
                                [("RING", sbr)] + brkeys)
                    t = self.rr("gt", 2)
                    hb = self.HBG[:, (l * 4 + i) * 8 + oc:(l * 4 + i) * 8 + oc + 1]
                    S.op("act", lambda e, t=t, bg=bg, hb=hb: e.activation(out=self.GT[t][:], in_=PS[bg][:], func=AF.Tanh,
                                                                          bias=hb, scale=0.5),
                         reads=[("PS", bg), "HBG"], writes=[("GT", t)])
                    if n_i == 0:
                        S.op("dve", lambda e, t=t, bp=bp, oc=oc: e.scalar_tensor_tensor(
                            out=self.MERGED[:, oc, :], in0=self.GT[t][:], scalar=1.0, in1=PS[bp][:],
                            op0=ALU.add, op1=ALU.mult),
                             reads=[("GT", t), ("PS", bp)], writes=[("MERGED", oc), "AL"])
                    else:
                        pr = self.rr("prd", 2)
                        S.op("dve", lambda e, t=t, bp=bp, pr=pr: e.scalar_tensor_tensor(
                            out=self.PRD[pr][:], in0=self.GT[t][:], scalar=1.0, in1=PS[bp][:],
                            op0=ALU.add, op1=ALU.mult),
                             reads=[("GT", t), ("PS", bp)], writes=[("PRD", pr)])
                        if n_i < 3:
                            S.op("pool", lambda e, pr=pr, oc=oc: e.tensor_tensor(
                                out=self.MERGED[:, oc, :], in0=self.MERGED[:, oc, :], in1=self.PRD[pr][:], op=ALU.add),
                                 reads=[("PRD", pr), ("MERGED", oc)], writes=[("MERGED", oc), "AL"])
                        else:
                            S.op("pool", lambda e, pr=pr, oc=oc: e.tensor_tensor(
                                out=self.MERGEDB[:, oc, :], in0=self.MERGED[:, oc, :], in1=self.PRD[pr][:], op=ALU.add),
                                 reads=[("PRD", pr), ("MERGED", oc)], writes=[("MERGEDB", oc)], guards=["AL"])
        S.op("act", lambda e: e.activation(out=self.DUM[:, 1:2], in_=self.DUM[:, 0:1], func=AF.Ln),
             reads=["DUM"], writes=["DUM1"])
        for q in range(4):
            so = self.wload(self.ws_wo[l, q], 2048, ("WS", "wo", l, q))
            WO = self.ringv(so, 8, 256)
            for o2 in range(2):
                oc = 2 * q + o2
                bk = self.pbank()
                for kc in range(8):
                    self.mm(bk, PS[bk][:], WO[:, kc, o2 * 128:(o2 + 1) * 128], self.MERGEDB[:, kc, :], kc == 0, kc == 7,
                            [("RING", so), ("MERGEDB", kc)])
                S.op("dve", lambda e, bk=bk, oc=oc: e.scalar_tensor_tensor(
                    out=X[:, oc, :], in0=PS[bk][:], scalar=RES_SCALE, in1=X[:, oc, :], op0=ALU.mult, op1=ALU.add),
                     reads=[("PS", bk), ("X", self.xpar, oc)], writes=[("X", self.xpar, oc)])
                if oc >= 1:
                    self.ln_chunk(oc - 1)
        self.ln_chunk(7)
        self.ln_finish(l, 1)
        if after is not None:
            after()

    def fourier(self, l, si):
        S = self.S
        PS = self.PS
        start, Sq = SEQS[si]
        N2 = Sq // 128
        P2 = 128 // N2
        E = N2 * 256
        blocks = range(start // TB, (start + Sq) // TB)
        ZTM = self.ZTM[:, 0:E].rearrange("p (s c) -> p s c", s=N2)
        T1 = self.T1[:, 0:E].rearrange("p (s n) -> p s n", s=N2)
        T2 = self.T2B[Sq]
        S.dma("sp", lambda e: e.dma_start(out=self.ZTM[:, 0:E],
                                          in_=self.zfs[l][start:start + Sq, :].rearrange("(p s) f -> p (s f)", p=128)),
              reads=[("ZFS", l, b) for b in blocks], writes=["ZTM"])
        S.dma("sp", lambda e: e.dma_start(out=self.T1[:, 0:E], in_=self.ws_t1[Sq]), reads=[("WS", "t1", Sq)],
              writes=["T1"])
        YL5 = self.YL[:, 0:E].rearrange("p (r g s j) -> p r g s j", r=2, g=N2, s=N2, j=P2)
        YT = self.YT[:, 0:E].rearrange("p (r g c) -> p r g c", r=2, g=N2)
        XF5 = self.XF[:, 0:2 * Sq].rearrange("p (r k g j) -> p r k g j", r=2, k=N2, g=N2, j=P2)
        XF = self.XF[:, 0:2 * Sq].rearrange("p (r n) -> p r n", r=2)
        scale = 1.0 / math.sqrt(64.0 * Sq)
        for ct in range(2):
            for s2 in range(N2):
                if s2 % 2 == 0:
                    bk = self.pbank()
                co = (s2 % 2) * 256
                self.mm(bk, PS[bk][:, co:co + 256], ZTM[:, s2, ct * 128:(ct + 1) * 128], T1[:, s2, :], True, True,
                        ["ZTM", "T1"])
                if s2 % 2 == 1:
                    eng = "act" if (s2 // 2) % 2 == 0 else "dve"
                    for hh in range(2):
                        src = PS[bk][:, hh * 256:(hh + 1) * 256].rearrange("p (r g j) -> p r g j", r=2, g=N2, j=P2)
                        dst = YL5[:, :, :, s2 - 1 + hh, :]
                        if eng == "act":
                            S.op("act", lambda e, src=src, dst=dst: e.copy(out=dst, in_=src), reads=[("PS", bk)],
                                 writes=["YL"])
                        else:
                            S.op("dve", lambda e, src=src, dst=dst: e.tensor_copy(out=dst, in_=src),
                                 reads=[("PS", bk)], writes=["YL"])
            for r in range(2):
                for g0 in range(0, N2, 8):
                    bk = self.pbank()
                    PSB = PS[bk][:].bitcast(BF16)
                    for t in range(8):
                        g = g0 + t
                        off = (r * N2 + g) * 128
                        S.op("pe", lambda e, PSB=PSB, t=t, off=off: e.transpose(PSB[:, t * 128:(t + 1) * 128],
                                                                                 self.YL[:, off:off + 128], self.IDB[:]),
                             reads=["YL", "IDB"], writes=[("PS", bk)])
                    eng = "act" if (g0 // 8) % 2 == 0 else "dve"
                    src = PSB.rearrange("p (t c) -> p t c", t=8)
                    dst = YT[:, r, g0:g0 + 8, :]
                    if eng == "act":
                        S.op("act", lambda e, src=src, dst=dst: e.copy(out=dst, in_=src), reads=[("PS", bk)], writes=["YT"])
                    else:
                        S.op("dve", lambda e, src=src, dst=dst: e.tensor_copy(out=dst, in_=src), reads=[("PS", bk)],
                             writes=["YT"])
            for g in range(N2):
                if g % 2 == 0:
                    bk = self.pbank()
                co = (g % 2) * 256
                self.mm(bk, PS[bk][:, co:co + 256], YT[:, 0, g, :], T2[:, 0, :], True, False, ["YT", ("T2B", Sq)])
                self.mm(bk, PS[bk][:, co:co + 256], YT[:, 1, g, :], T2[:, 1, :], False, True, ["YT", ("T2B", Sq)])
                if g % 2 == 1:
                    eng = "act" if (g // 2) % 2 == 0 else "dve"
                    for hh in range(2):
                        src = PS[bk][:, hh * 256:(hh + 1) * 256].rearrange("p (j r k) -> p r k j", j=P2, r=2, k=N2)
                        dst = XF5[:, :, :, g - 1 + hh, :]
                        if eng == "act":
                            S.op("act", lambda e, src=src, dst=dst: e.copy(out=dst, in_=src), reads=[("PS", bk)],
                                 writes=["XF"])
                        else:
                            S.op("dve", lambda e, src=src, dst=dst: e.tensor_copy(out=dst, in_=src),
                                 reads=[("PS", bk)], writes=["XF"])
            for ks in range(Sq // 512):
                bk = self.pbank()
                self.mm(bk, PS[bk][:], self.BDCB[:], XF[:, 0, ks * 512:(ks + 1) * 512], True, False, ["XF", "BDCB"])
                self.mm(bk, PS[bk][:], self.BDSB[:], XF[:, 1, ks * 512:(ks + 1) * 512], False, True, ["XF", "BDSB"])
                S.op("act", lambda e, bk=bk, ks=ks: e.mul(out=self.FO[:, ks * 512:(ks + 1) * 512], in_=PS[bk][:],
                                                          mul=scale), reads=[("PS", bk)], writes=["FO"])
            S.dma("sp", lambda e, ct=ct: e.dma_start(out=self.brfs[l][ct][:, start:start + Sq], in_=self.FO[:, 0:Sq]),
                  reads=["FO"], writes=[("BRFS", l, si, ct)])

    def load_input_block(self, b):
        S = self.S
        PS = self.PS
        Xl = self.X
        XBl = self.XBs[self.par]
        tok0 = b * TB
        for c4 in range(4):
            xi = self.rr("xio", 3)
            S.dma("sp", lambda e, xi=xi, c4=c4: e.dma_start(out=self.XIO[xi][:],
                                                           in_=self.xin[tok0 + c4 * 128:tok0 + (c4 + 1) * 128, :]),
                  writes=[("XIO", xi)])
            for a in range(2):
                bk = self.pbank()
                for kk in range(4):
                    kc = 4 * a + kk
                    S.op("pe", lambda e, bk=bk, kk=kk, kc=kc, xi=xi: e.transpose(
                        PS[bk][:, kk * 128:(kk + 1) * 128], self.XIO[xi][:, kc * 128:(kc + 1) * 128], self.IDENT[:]),
                         reads=[("XIO", xi), "IDENT"], writes=[("PS", bk)])
                src = PS[bk][:].rearrange("p (k n) -> p k n", k=4)
                S.op("dve", lambda e, src=src, a=a, c4=c4: e.tensor_copy(
                    out=Xl[:, 4 * a:4 * a + 4, c4 * 128:(c4 + 1) * 128], in_=src),
                     reads=[("PS", bk)], writes=[("X", self.xpar, 4 * a + kk) for kk in range(4)])
                S.op("act", lambda e, src=src, a=a, c4=c4: e.copy(
                    out=XBl[:, 4 * a:4 * a + 4, c4 * 128:(c4 + 1) * 128], in_=src),
                     reads=[("PS", bk)], writes=[("XB", self.par, 4 * a + kk) for kk in range(4)])

    def store_output_block(self, b):
        S = self.S
        PS = self.PS
        Xl = self.X
        tok0 = b * TB
        for c4 in range(4):
            xi = self.rr("xio", 3)
            for a in range(2):
                bk = self.pbank()
                for kk in range(4):
                    kc = 4 * a + kk
                    S.op("pe", lambda e, bk=bk, kk=kk, kc=kc, c4=c4: e.transpose(
                        PS[bk][:, kk * 128:(kk + 1) * 128], Xl[:, kc, c4 * 128:(c4 + 1) * 128], self.IDENT[:]),
                         reads=[("X", self.xpar, kc), "IDENT"], writes=[("PS", bk)])
                if a == 0:
                    S.op("dve", lambda e, bk=bk, xi=xi, a=a: e.tensor_copy(out=self.XIO[xi][:, a * 512:(a + 1) * 512],
                                                                          in_=PS[bk][:]),
                         reads=[("PS", bk)], writes=[("XIO", xi)])
                else:
                    S.op("act", lambda e, bk=bk, xi=xi, a=a: e.copy(out=self.XIO[xi][:, a * 512:(a + 1) * 512],
                                                                   in_=PS[bk][:]),
                         reads=[("PS", bk)], writes=[("XIO", xi)])
            S.dma("act", lambda e, xi=xi, c4=c4: e.dma_start(out=self.yout[tok0 + c4 * 128:tok0 + (c4 + 1) * 128, :],
                                                            in_=self.XIO[xi][:]),
                  reads=[("XIO", xi)], final=True)

    def block_info(self, b):
        tok0 = b * TB
        for si, (st, Sq) in enumerate(SEQS):
            if st <= tok0 < st + Sq:
                return si, tok0 == st, tok0 + TB == st + Sq
        raise AssertionError

    def build(self, stage=None):
        S = self.S
        self.prologue()
        if stage == "pro":
            return S.emit()
        per = (len(self.late_casts) + NBLK - 1) // NBLK
        for b in range(NBLK):
            for o in self.late_casts[b * per:(b + 1) * per]:
                o.idx = len(S.ops)
                S.ops.append(o)
            self.par = b % 2
            self.xpar = b % 2
            self.load_input_block(b)
            if stage == "p0_load":
                return S.emit()
            self.ffn(0, 0, (0, 0))
            self.zf_and_spill(0, b)
            if stage == "p0b1":
                return S.emit()
        self.xpar = 0
        if stage == "p0":
            return S.emit()
        for l in range(DEPTH):
            S.barrier()
            for si in range(len(SEQS)):
                self.fourier(l, si)
                if stage == "f0s1":
                    return S.emit()
            S.barrier()
            if stage == "f0":
                return S.emit()
            for b in range(NBLK):
                if stage == "p1b1" and b == 1:
                    return S.emit()
                if stage == "p1b2" and b == 2:
                    return S.emit()
                si, first, last = self.block_info(b)
                self.par = b % 2
                if l == 0:
                    per2 = (len(self.late_casts2) + NBLK - 1) // NBLK
                    for o in self.late_casts2[b * per2:(b + 1) * per2]:
                        o.idx = len(S.ops)
                        S.ops.append(o)
                if b == 0:
                    self.load_block(l, b, first, last)
                    self.mix_part1(l, self.par, (0, 2, 6, 1, 4, 5))
                pA = pB = pC = pBC = None
                if b + 1 < NBLK:
                    nsi, nfirst, nlast = self.block_info(b + 1)
                    npar = (b + 1) % 2
                    self.par = npar
                    self.load_block(l, b + 1, nfirst, nlast)
                    self.par = b % 2
                    pA = lambda l=l, npar=npar: self.mix_part1(l, npar, (0, 2))
                    pB = lambda l=l, npar=npar: self.mix_part1(l, npar, (6, 1))
                    pC = lambda l=l, npar=npar: self.mix_part1(l, npar, (4, 5))
                    pBC = lambda l=l, npar=npar: self.mix_part1(l, npar, (6, 1, 4, 5))
                self.mix(l, b, si, first, last, after=pA)
                if l + 1 < DEPTH:
                    self.ffn(l, 1, (l, 2), after=pB)
                    self.ffn(l + 1, 0, (l + 1, 0), after=pC)
                    self.zf_and_spill(l + 1, b)
                else:
                    self.ffn(l, 1, (l, 2), after=pBC)
                    self.store_output_block(b)
        n = S.emit()
        return n


_CACHE = {}


def _get_program():
    if "nc" not in _CACHE:
        bld = Builder()
        n = bld.build()
        _CACHE["nc"] = bld.nc
        _CACHE["n"] = n
    return _CACHE["nc"]


def kernel(x_prompt, x_sample, ln_g, ln_b, ffn_w1, ffn_w3, ffn_w2, w_in, conv_w, sgu_ln_g, sgu_ln_b, sgu_w,
           sgu_b, pool_w, pool_scale, w_gate, b_gate, w_branch, w_out):
    f = lambda a: np.ascontiguousarray(np.asarray(a, dtype=np.float32))
    x_prompt = f(x_prompt)
    x_sample = f(x_sample)
    shared = dict(ln_g=f(ln_g), ln_b=f(ln_b), ffn_w1=f(ffn_w1), ffn_w3=f(ffn_w3), ffn_w2=f(ffn_w2), w_in=f(w_in),
                  conv_w=f(conv_w), sgu_ln_g=f(sgu_ln_g), sgu_ln_b=f(sgu_ln_b), sgu_w=f(sgu_w), sgu_b=f(sgu_b),
                  pool_w=f(pool_w), pool_scale=f(pool_scale), w_gate=f(w_gate), b_gate=f(b_gate),
                  w_branch=f(w_branch), w_out=f(w_out))
    shared.update(_const_inputs())
    nc = _get_program()
    in_maps = []
    for c in range(NCORES):
        xin = np.concatenate([x_prompt[2 * c].reshape(4096, D), x_prompt[2 * c + 1].reshape(4096, D),
                              x_sample[c].reshape(2048, D)], axis=0)
        m = dict(shared)
        m["xin"] = np.ascontiguousarray(xin)
        in_maps.append(m)
    res = run_bass_kernel_spmd(nc, in_maps, core_ids=list(range(NCORES)))
    y_prompt = np.empty((16, 4096, D), dtype=np.float32)
    y_sample = np.empty((8, 2048, D), dtype=np.float32)
    for c in range(NCORES):
        y = np.asarray(res.results[c]["yout"], dtype=np.float32)
        y_prompt[2 * c] = y[0:4096]
        y_prompt[2 * c + 1] = y[4096:8192]
        y_sample[c] = y[8192:10240]
    _CACHE["last"] = res
    return (y_prompt, y_sample)
```
